# Optimizing a Trainium2 kernel written in Bass

```python
import jax, jax.numpy as jnp
from jax import lax
import numpy as np

D_MODEL = 2048
BATCH = 2
SEQ = 16384
DEPTH = 2

HEAD_DIM = 128
A_GROUPS = ((128, 1), (512, 4), (2048, 16))
A_HEADS_PER_GROUP = 4
A_HEADS = A_HEADS_PER_GROUP * len(A_GROUPS)
A_WIDTH = A_HEADS * HEAD_DIM
A_OUT = A_HEADS_PER_GROUP * HEAD_DIM
B_HEADS = 8
B_WIDTH = B_HEADS * HEAD_DIM
IDX_HEADS = 16
IDX_DIM = 64
IDX_TOPK = 256
BLOCK = 128
MLP_HIDDEN = 4 * D_MODEL
ROPE_THETA = 10000.0
EPS = 1e-6
N_MOD = 6
N_IN = 3 * A_WIDTH + 3 * B_WIDTH + IDX_HEADS * IDX_DIM + IDX_DIM + IDX_HEADS

kernel_name = "hybrid_dilated_dsa_adaln_block"


def rms_norm(x, gain=None):
    xf = x.astype(jnp.float32)
    y = xf * lax.rsqrt(jnp.mean(xf * xf, axis=-1, keepdims=True) + EPS)
    if gain is not None:
        y = y * gain.astype(jnp.float32)
    return y.astype(x.dtype)


def rope(x, positions):
    d = x.shape[-1]
    half = d // 2
    inv_freq = jnp.power(ROPE_THETA, -jnp.arange(half, dtype=jnp.float32) * 2.0 / d)
    ang = positions.astype(jnp.float32)[..., None] * inv_freq
    cos = jnp.cos(ang)[:, :, None, :]
    sin = jnp.sin(ang)[:, :, None, :]
    xf = x.astype(jnp.float32)
    x1, x2 = xf[..., :half], xf[..., half:]
    return jnp.concatenate([x1 * cos - x2 * sin, x2 * cos + x1 * sin], axis=-1).astype(x.dtype)


def dilated_attention(q, k, v, window, dilation):
    b, s, h, d = q.shape
    r = dilation
    span = window // dilation
    blk = BLOCK
    chunk = r * blk
    sp = -(-s // chunk) * chunk
    m = sp // r
    nb = m // blk

    def to_blocks(t):
        t = jnp.pad(t, ((0, 0), (0, sp - s), (0, 0), (0, 0)))
        t = t.reshape(b, m, r, h, d).transpose(0, 2, 1, 3, 4)
        return t.reshape(b, r, nb, blk, h, d)

    def with_prev(t):
        prev = jnp.pad(t, ((0, 0), (0, 0), (1, 0), (0, 0), (0, 0), (0, 0)))[:, :, :-1]
        return jnp.concatenate([prev, t], axis=3)

    qb = to_blocks(q)
    kk = with_prev(to_blocks(k))
    vv = with_prev(to_blocks(v))
    scores = jnp.einsum('bpnqhd,bpnkhd->bpnhqk', qb, kk).astype(jnp.float32) * (d ** -0.5)
    qi = jnp.arange(blk)[:, None] + blk
    ki = jnp.arange(2 * blk)[None, :]
    dist = qi - ki
    key_idx = jnp.arange(nb)[:, None, None] * blk + ki - blk
    valid = (dist >= 0) & (dist <= span) & (key_idx >= 0)
    scores = jnp.where(valid[:, None], scores, -jnp.inf)
    mx = jnp.max(scores, axis=-1, keepdims=True)
    e = jnp.exp(scores - mx)
    den = jnp.sum(e, axis=-1)
    lse = jnp.swapaxes(mx[..., 0] + jnp.log(den), -1, -2)
    out = jnp.einsum('bpnhqk,bpnkhd->bpnqhd', e.astype(vv.dtype), vv).astype(jnp.float32)
    out = out / jnp.swapaxes(den, -1, -2)[..., None]

    def from_blocks(t):
        t = t.reshape((b, r, m) + t.shape[4:])
        t = jnp.swapaxes(t, 1, 2)
        return t.reshape((b, sp) + t.shape[3:])[:, :s]

    return from_blocks(out), from_blocks(lse)


def dsa_attention(q, k, v, q_idx, k_idx, w_idx):
    b, s, h, d = q.shape
    topk = min(IDX_TOPK, s // 4)
    nb = s // BLOCK
    key_pos = jnp.arange(s)
    gather = jax.vmap(lambda t, i: t[i])

    def one_block(i):
        start = i * BLOCK
        qi = lax.dynamic_slice_in_dim(q_idx, start, BLOCK, axis=1)
        wi = lax.dynamic_slice_in_dim(w_idx, start, BLOCK, axis=1)
        qb = lax.dynamic_slice_in_dim(q, start, BLOCK, axis=1)
        qpos = start + jnp.arange(BLOCK)
        logits = jnp.einsum('bqhd,bsd->bqhs', qi, k_idx).astype(jnp.float32) * (IDX_DIM ** -0.5)
        score = jnp.einsum('bqh,bqhs->bqs', wi.astype(jnp.float32), jax.nn.relu(logits))
        causal = key_pos[None, :] <= qpos[:, None]
        score = jnp.where(causal[None], score, -jnp.inf)
        _, sel = lax.top_k(score, topk)
        ks = gather(k, sel)
        vs = gather(v, sel)
        att = jnp.einsum('bqhd,bqkhd->bhqk', qb, ks).astype(jnp.float32) * (d ** -0.5)
        ok = sel <= qpos[None, :, None]
        att = jnp.where(ok[:, None], att, -jnp.inf)
        p = jax.nn.softmax(att, axis=-1)
        return jnp.einsum('bhqk,bqkhd->bqhd', p.astype(vs.dtype), vs)

    out = lax.map(one_block, jnp.arange(nb))
    return jnp.moveaxis(out, 0, 1).reshape(b, s, h, d)


def token_mixer(h, positions, w_in, a_q_gain, a_k_gain, b_q_gain, b_k_gain, idx_k_gain,
                w_gate, b_gate, w_proj_a, w_proj_b, w_out):
    b, s, _ = h.shape
    z = h @ w_in
    sizes = (A_WIDTH, A_WIDTH, A_WIDTH, B_WIDTH, B_WIDTH, B_WIDTH,
             IDX_HEADS * IDX_DIM, IDX_DIM, IDX_HEADS)
    cuts = []
    acc = 0
    for sz in sizes[:-1]:
        acc += sz
        cuts.append(acc)
    aq, ak, av, bq, bk, bv, iq, ik, iw = jnp.split(z, cuts, axis=-1)

    aq = rope(rms_norm(aq.reshape(b, s, A_HEADS, HEAD_DIM), a_q_gain), positions)
    ak = rope(rms_norm(ak.reshape(b, s, A_HEADS, HEAD_DIM), a_k_gain), positions)
    av = av.reshape(b, s, A_HEADS, HEAD_DIM)
    outs, lses = [], []
    for g, (win, dil) in enumerate(A_GROUPS):
        sl = slice(g * A_HEADS_PER_GROUP, (g + 1) * A_HEADS_PER_GROUP)
        o, lse = dilated_attention(aq[:, :, sl], ak[:, :, sl], av[:, :, sl], win, dil)
        outs.append(o)
        lses.append(lse)
    wts = jax.nn.softmax(jnp.stack(lses), axis=0)
    o_a = jnp.sum(wts[..., None] * jnp.stack(outs), axis=0).astype(h.dtype).reshape(b, s, A_OUT)

    bq = rope(rms_norm(bq.reshape(b, s, B_HEADS, HEAD_DIM), b_q_gain), positions)
    bk = rope(rms_norm(bk.reshape(b, s, B_HEADS, HEAD_DIM), b_k_gain), positions)
    bv = bv.reshape(b, s, B_HEADS, HEAD_DIM)
    iq = rope(iq.reshape(b, s, IDX_HEADS, IDX_DIM), positions)
    ik = rope(rms_norm(ik, idx_k_gain)[:, :, None, :], positions)[:, :, 0, :]
    iw = iw * (IDX_HEADS ** -0.5)
    o_b = dsa_attention(bq, bk, bv, iq, ik, iw).reshape(b, s, B_WIDTH)

    gates = jax.nn.sigmoid((h @ w_gate + b_gate).astype(jnp.float32)).astype(h.dtype)
    g_a, g_b = jnp.split(gates, 2, axis=-1)
    merged = g_a * (o_a @ w_proj_a) + g_b * (o_b @ w_proj_b)
    return merged @ w_out


def setup_inputs(seed: int = 0) -> dict:
    key = jax.random.key(seed)
    ks = jax.random.split(key, 20)
    f32 = jnp.float32

    def nrm(k, shape, scale):
        return jax.random.normal(k, shape, f32) * scale

    def gain(k, n):
        return 1.0 + 0.02 * jax.random.normal(k, (DEPTH, n), f32)

    x = jax.random.normal(ks[0], (BATCH, SEQ, D_MODEL), f32)
    c = jax.random.normal(ks[1], (BATCH, D_MODEL), f32)
    offset = jax.random.randint(ks[2], (BATCH, 1), 0, 1024, dtype=jnp.int32)
    positions = (jnp.arange(SEQ, dtype=jnp.int32)[None, :] + offset).astype(jnp.int32)
    return {
        "x": x,
        "c": c,
        "positions": positions,
        "w_ada": nrm(ks[3], (DEPTH, D_MODEL, N_MOD * D_MODEL), D_MODEL ** -0.5),
        "b_ada": nrm(ks[4], (DEPTH, N_MOD * D_MODEL), 0.01),
        "w_in": nrm(ks[5], (DEPTH, D_MODEL, N_IN), D_MODEL ** -0.5),
        "a_q_gain": gain(ks[6], HEAD_DIM),
        "a_k_gain": gain(ks[7], HEAD_DIM),
        "b_q_gain": gain(ks[8], HEAD_DIM),
        "b_k_gain": gain(ks[9], HEAD_DIM),
        "idx_k_gain": gain(ks[10], IDX_DIM),
        "w_gate": nrm(ks[11], (DEPTH, D_MODEL, 2 * D_MODEL), D_MODEL ** -0.5),
        "b_gate": nrm(ks[12], (DEPTH, 2 * D_MODEL), 0.01),
        "w_proj_a": nrm(ks[13], (DEPTH, A_OUT, D_MODEL), A_OUT ** -0.5),
        "w_proj_b": nrm(ks[14], (DEPTH, B_WIDTH, D_MODEL), B_WIDTH ** -0.5),
        "w_out": nrm(ks[15], (DEPTH, D_MODEL, D_MODEL), D_MODEL ** -0.5),
        "w_up": nrm(ks[16], (DEPTH, D_MODEL, MLP_HIDDEN), D_MODEL ** -0.5),
        "w_down": nrm(ks[17], (DEPTH, MLP_HIDDEN, D_MODEL), MLP_HIDDEN ** -0.5),
    }


def reference(x, c, positions, w_ada, b_ada, w_in, a_q_gain, a_k_gain, b_q_gain, b_k_gain,
              idx_k_gain, w_gate, b_gate, w_proj_a, w_proj_b, w_out, w_up, w_down):
    c_act = jax.nn.silu(c)
    for l in range(DEPTH):
        mod = c_act @ w_ada[l] + b_ada[l]
        sh1, sc1, g1, sh2, sc2, g2 = jnp.split(mod, N_MOD, axis=-1)
        h = rms_norm(x) * (1.0 + sc1[:, None]) + sh1[:, None]
        mix = token_mixer(h, positions, w_in[l], a_q_gain[l], a_k_gain[l], b_q_gain[l],
                          b_k_gain[l], idx_k_gain[l], w_gate[l], b_gate[l],
                          w_proj_a[l], w_proj_b[l], w_out[l])
        x = x + g1[:, None] * mix
        h = rms_norm(x) * (1.0 + sc2[:, None]) + sh2[:, None]
        ffn = jnp.square(jax.nn.relu(h @ w_up[l])) @ w_down[l]
        x = x + g2[:, None] * ffn
    return x
```

```python
from contextlib import ExitStack
import numpy as np
import concourse.bass as bass
import concourse.mybir as mybir

F32 = mybir.dt.float32
BF16 = mybir.dt.bfloat16
I32 = mybir.dt.int32
ALU = mybir.AluOpType
AF = mybir.ActivationFunctionType

ENGS = ("pe", "act", "dve", "pool", "sp")
NDMA = 8


class Prog:
    def __init__(self, nc, stack):
        self.nc = nc
        self.stack = stack
        self.ops = []
        self.lastw = {}
        self.readers = {}
        self.eng_ops = {e: [] for e in ENGS}
        self.ndma = {e: 0 for e in ENGS}
        self.sems = {e: stack.enter_context(nc.semaphore("s_" + e)) for e in ENGS}
        self.dsems = {}
        for e in ("sp", "pool", "act"):
            self.dsems[e] = [stack.enter_context(nc.semaphore("d_%s%d" % (e, i))) for i in range(NDMA)]
        self._n = 0

    def sb(self, shape, dtype, name=None):
        self._n += 1
        return self.stack.enter_context(self.nc.sbuf_tensor("s_" + (name or ("sb%d" % self._n)), list(shape), dtype))

    def ps(self, shape, dtype=F32, name=None):
        self._n += 1
        return self.stack.enter_context(self.nc.psum_tensor("p_" + (name or ("ps%d" % self._n)), list(shape), dtype))

    def op(self, eng, fn, reads=(), writes=(), dma=False):
        deps = {}
        for k in reads:
            w = self.lastw.get(k)
            if w is not None:
                deps[w] = "raw"
        for k in writes:
            w = self.lastw.get(k)
            if w is not None and w not in deps:
                deps[w] = "waw"
            for r in self.readers.get(k, ()):
                if r not in deps:
                    deps[r] = "war"
        idx = len(self.ops)
        o = dict(eng=eng, fn=fn, deps=deps, dma=dma, seq=len(self.eng_ops[eng]), idx=idx)
        if dma:
            n = self.ndma[eng]
            self.ndma[eng] += 1
            o["dslot"] = n % NDMA
            o["dval"] = 16 * (n // NDMA + 1)
        self.ops.append(o)
        self.eng_ops[eng].append(o)
        for k in reads:
            self.readers.setdefault(k, []).append(idx)
        for k in writes:
            self.lastw[k] = idx
            self.readers[k] = []
        return idx

    def dma(self, eng, out, in_, reads=(), writes=(), **kw):
        return self.op(eng, lambda e: e.dma_start(out=out, in_=in_, **kw), reads, writes, dma=True)

    def emit(self):
        nc = self.nc
        for e in ENGS:
            seen = {}
            for o in self.eng_ops[e]:
                waits = []
                if o["dma"] and o["dval"] > 16:
                    key = ("d", e, o["dslot"])
                    if seen.get(key, 0) < o["dval"] - 16:
                        waits.append((key, o["dval"] - 16, None))
                        seen[key] = o["dval"] - 16
                for d, typ in o["deps"].items():
                    p = self.ops[d]
                    if p["dma"]:
                        key = ("d", p["eng"], p["dslot"])
                        if seen.get(key, 0) < p["dval"]:
                            waits.append((key, p["dval"], None))
                            seen[key] = p["dval"]
                    else:
                        if p["eng"] == e:
                            if e == "pe" or typ != "raw":
                                continue
                        key = ("e", p["eng"])
                        if seen.get(key, -1) < p["seq"]:
                            waits.append((key, None, p))
                            seen[key] = p["seq"]
                            p["marked"] = True
                o["waits"] = waits
        for e in ENGS:
            c = 0
            for o in self.eng_ops[e]:
                if o.get("marked"):
                    c += 1
                    o["count"] = c
        def run(e, eng):
            for o in self.eng_ops[e]:
                for key, val, p in o["waits"]:
                    if key[0] == "d":
                        eng.wait_ge(self.dsems[key[1]][key[2]], val)
                    else:
                        eng.wait_ge(self.sems[key[1]], p["count"])
                ins = o["fn"](eng)
                if o["dma"]:
                    ins.then_inc(self.dsems[e][o["dslot"]], 16)
                elif o.get("marked"):
                    ins.then_inc(self.sems[e], 1)
            if self.ndma[e]:
                n = self.ndma[e]
                for s in range(min(n, NDMA)):
                    last = 16 * ((n - 1 - s) // NDMA + 1)
                    eng.wait_ge(self.dsems[e][s], last)

        with nc.Block() as block:
            @block.sync
            def _(eng):
                run("sp", eng)

            @block.tensor
            def _(eng):
                run("pe", eng)

            @block.scalar
            def _(eng):
                run("act", eng)

            @block.vector
            def _(eng):
                run("dve", eng)

            @block.gpsimd
            def _(eng):
                run("pool", eng)
        print("ops:", {e: len(self.eng_ops[e]) for e in ENGS}, flush=True)

import numpy as np
from contextlib import ExitStack
import concourse.bass as bass
import concourse.mybir as mybir

D = 2048
KC = 16
TB = 512
EPS = 1e-6
NWB = 6
WC = 256


class Env:
    def __init__(self, P, npsum=6):
        self.P = P
        self.wb = [(P.sb([128, KC, WC], BF16, "wt%d" % i), "wt%d" % i) for i in range(NWB)]
        self.wi = 0
        self.pb = [(P.ps([128, 512], F32, "pb%d" % i), "pb%d" % i) for i in range(npsum)]
        self.pi = 0

    def wtile(self, W, r0, nk, c0, ncols=WC):
        wt, key = self.wb[self.wi % NWB]
        self.wi += 1
        src = W[r0:r0 + nk * 128, c0:c0 + ncols].rearrange("(kc p) c -> p kc c", p=128)
        self.P.dma("pool", wt[:, 0:nk, 0:ncols], src, writes=(key,))
        return wt, key

    def bank(self):
        b = self.pb[self.pi % len(self.pb)]
        self.pi += 1
        return b

    def gemm(self, W, nk, rhs, rkey, c0, ncols, epi, N=TB):
        P = self.P
        nch = (ncols + 127) // 128
        for cp in range(0, nch, 2):
            chs = [j for j in (cp, cp + 1) if j < nch]
            banks = [self.bank() for _ in chs]
            w = min(WC, ncols - cp * 128)
            for kg in range(0, nk, KC):
                nkk = min(KC, nk - kg)
                wt, wk = self.wtile(W, kg * 128, nkk, c0 + cp * 128, w)
                for mi, j in enumerate(chs):
                    mw = min(128, ncols - j * 128)
                    pt, pk = banks[mi]
                    for k in range(nkk):
                        kk = kg + k
                        P.op("pe", lambda e, pt=pt, wt=wt, k=k, mi=mi, mw=mw, kk=kk: e.matmul(
                            pt[0:mw, 0:N], wt[:, k, mi * 128:mi * 128 + mw], rhs(kk),
                            start=(kk == 0), stop=(kk == nk - 1)),
                            reads=(wk, rkey(kk)), writes=(pk,))
            for mi, j in enumerate(chs):
                epi(j, banks[mi][0], banks[mi][1])


def compute_mod(P, E, c_d, w_ada, b_ada, ch0, n, modT):
    cs = P.sb([128, KC], F32, "c_s")
    ca = P.sb([128, KC], BF16, "c_a")
    bt = P.sb([128, n], F32, "b_t")
    P.dma("sp", cs[:], c_d, writes=("cs",))
    P.dma("sp", bt[:], b_ada[:, ch0:ch0 + n], writes=("bt",))
    P.op("act", lambda e: e.activation(out=ca[:], in_=cs[:], func=AF.Silu), reads=("cs",), writes=("ca",))

    def epi(j, pt, pk):
        P.op("dve", lambda e: e.tensor_tensor(out=modT[:, j:j + 1], in0=pt[:, 0:1], in1=bt[:, j:j + 1], op=ALU.add),
             reads=(pk, "bt"), writes=("mod",))

    E.gemm(w_ada, KC, lambda k: ca[:, k:k + 1], lambda k: "ca", ch0 * 128, n * 128, epi, N=1)


def rms_mod(P, xT, xkey, hT, hkey, sq, sqk, ones, pss, rstd, tmp, eps, scp, shv):
    for k in range(KC):
        P.op("act", lambda e, k=k: e.activation(out=sq[:, k, :], in_=xT[:, k, :], func=AF.Square),
             reads=((xkey, k),), writes=(sqk(k),))
    for k in range(KC):
        P.op("pe", lambda e, k=k: e.matmul(pss[:], ones[:], sq[:, k, :], start=(k == 0), stop=(k == KC - 1)),
             reads=(sqk(k), "ones"), writes=("pss",))
    P.op("act", lambda e: e.activation(out=rstd[:], in_=pss[:], func=AF.Sqrt, scale=1.0 / D, bias=eps[:, 0:1]),
         reads=("pss", "eps"), writes=("rstd",))
    P.op("dve", lambda e: e.reciprocal(out=rstd[:], in_=rstd[:]), reads=("rstd",), writes=("rstd",))
    for k in range(KC):
        P.op("dve", lambda e, k=k: e.tensor_tensor(out=tmp[:, k % 2, :], in0=xT[:, k, :], in1=rstd[:], op=ALU.mult),
             reads=((xkey, k), "rstd"), writes=(("tmp", k % 2),))
        P.op("act", lambda e, k=k: e.activation(out=hT[:, k, :], in_=tmp[:, k % 2, :], func=AF.Identity,
                                             scale=scp[:, k:k + 1], bias=shv[:, k:k + 1]),
             reads=(("tmp", k % 2), "mod", "modp"), writes=((hkey, k),))


def build_L3(T):
    nc = bass.Bass("TRN2", target_bir_lowering=False)
    NB = T // TB
    dt = lambda n, s, d, k="ExternalInput": nc.dram_tensor(n, s, d, kind=k).ap()
    xT_d = dt("xT", [D, T], F32)
    oT_d = dt("oT", [1536, T], BF16)
    mod_d = dt("modi", [128, 96], F32)
    w_gate = dt("w_gate", [D, 2 * D], F32)
    b_gate = dt("b_gate", [128, 32], F32)
    w_pa = dt("w_proj_a", [512, D], F32)
    w_pb = dt("w_proj_b", [1024, D], F32)
    w_out = dt("w_out", [D, D], F32)
    w_up = dt("w_up", [D, 4 * D], F32)
    w_down = dt("w_down", [4 * D, D], F32)
    yT_d = dt("yT", [D, T], F32, "ExternalOutput")

    with ExitStack() as st:
        P = Prog(nc, st)
        E = Env(P)
        modT = P.sb([128, 96], F32, "modT")
        P.dma("sp", modT[:], mod_d, writes=("mod",))
        scp = P.sb([128, 2, KC], F32, "scp")
        P.op("dve", lambda e: e.tensor_scalar(out=scp[:, 0, :], in0=modT[:, 16:32], scalar1=1.0, scalar2=None, op0=ALU.add),
             reads=("mod",), writes=("modp",))
        P.op("dve", lambda e: e.tensor_scalar(out=scp[:, 1, :], in0=modT[:, 64:80], scalar1=1.0, scalar2=None, op0=ALU.add),
             reads=("mod", "modp"), writes=("modp",))
        eps = P.sb([128, 1], F32, "eps")
        P.op("dve", lambda e: e.memset(eps[:], EPS), writes=("eps",))
        ones = P.sb([128, 128], BF16, "ones")
        P.op("dve", lambda e: e.memset(ones[:], 1.0), writes=("ones",))
        bg = P.sb([128, 32], F32, "bg")
        P.dma("sp", bg[:], b_gate, writes=("bg",))

        xT = P.sb([128, KC, TB], F32, "xT")
        hT = P.sb([128, KC, TB], BF16, "hT")
        aT = P.sb([128, 64, TB], BF16, "aT")
        mg = aT[:, 0:16, :]
        sq = aT[:, 16:32, :]
        oT = aT[:, 32:44, :]
        rstd = P.sb([128, TB], F32, "rstd")
        tmp = P.sb([128, 2, TB], F32, "tmp")
        gs = P.sb([128, 4, TB], F32, "gs")
        pss = P.ps([128, 512], F32, "pss")

        class SQ:
            pass

        for b in range(NB):
            t0 = b * TB
            for k in range(KC):
                P.dma("sp", xT[:, k, :], xT_d[k * 128:(k + 1) * 128, t0:t0 + TB], writes=(("x", k),))
            for k in range(12):
                P.dma("sp", oT[:, k, :], oT_d[k * 128:(k + 1) * 128, t0:t0 + TB], writes=(("a", 32 + k),))
            rms_mod(P, xT, "x", hT, "h", aT[:, 16:32, :], (lambda k: ("a", 16 + k)), ones, pss, rstd, tmp, eps, scp[:, 0, :], modT[:, 0:16])
            st_ = {}

            def mk_epi(kind):
                def epi(j, pt, pk):
                    st_[(kind, j)] = (pt, pk)
                return epi
            for cp in range(0, KC, 2):
                E.gemm(w_gate, KC, lambda k: hT[:, k, :], lambda k: ("h", k), cp * 128, 256, mk_epi("ga"))
                for j in range(2):
                    ch = cp + j
                    pt, pk = st_[("ga", j)]
                    P.op("act", lambda e, pt=pt, ch=ch, j=j: e.activation(out=gs[:, j, :], in_=pt[:], func=AF.Sigmoid,
                                                                     bias=bg[:, ch:ch + 1]),
                         reads=(pk, "bg"), writes=(("gs", j),))
                E.gemm(w_gate, KC, lambda k: hT[:, k, :], lambda k: ("h", k), D + cp * 128, 256, mk_epi("gb"))
                for j in range(2):
                    ch = cp + j
                    pt, pk = st_[("gb", j)]
                    P.op("act", lambda e, pt=pt, ch=ch, j=j: e.activation(out=gs[:, 2 + j, :], in_=pt[:], func=AF.Sigmoid,
                                                                     bias=bg[:, 16 + ch:17 + ch]),
                         reads=(pk, "bg"), writes=(("gs", 2 + j),))
                E.gemm(w_pa, 4, lambda k: oT[:, k, :], lambda k: ("a", 32 + k), cp * 128, 256, mk_epi("pa"))
                for j in range(2):
                    pt, pk = st_[("pa", j)]
                    P.op("dve", lambda e, pt=pt, j=j: e.tensor_tensor(out=gs[:, j, :], in0=gs[:, j, :], in1=pt[:], op=ALU.mult),
                         reads=(pk, ("gs", j)), writes=(("gs", j),))
                E.gemm(w_pb, 8, lambda k: oT[:, 4 + k, :], lambda k: ("a", 36 + k), cp * 128, 256, mk_epi("pb"))
                for j in range(2):
                    ch = cp + j
                    pt, pk = st_[("pb", j)]
                    P.op("dve", lambda e, pt=pt, j=j: e.tensor_tensor(out=gs[:, 2 + j, :], in0=gs[:, 2 + j, :], in1=pt[:], op=ALU.mult),
                         reads=(pk, ("gs", 2 + j)), writes=(("gs", 2 + j),))
                    P.op("dve", lambda e, j=j, ch=ch: e.tensor_tensor(out=mg[:, ch, :], in0=gs[:, j, :], in1=gs[:, 2 + j, :], op=ALU.add),
                         reads=(("gs", j), ("gs", 2 + j)), writes=(("a", ch),))

            def epi_out(j, pt, pk):
                P.op("dve", lambda e: e.scalar_tensor_tensor(out=xT[:, j, :], in0=pt[:], scalar=modT[:, 32 + j:33 + j],
                                                             in1=xT[:, j, :], op0=ALU.mult, op1=ALU.add),
                     reads=(pk, "mod", ("x", j)), writes=(("x", j),))
            E.gemm(w_out, KC, lambda k: mg[:, k, :], lambda k: ("a", k), 0, D, epi_out)
            rms_mod(P, xT, "x", hT, "h", aT[:, 16:32, :], (lambda k: ("a", 16 + k)), ones, pss, rstd, tmp, eps, scp[:, 1, :], modT[:, 48:64])

            def epi_up(j, pt, pk):
                P.op("act", lambda e: e.activation(out=tmp[:, j % 2, :], in_=pt[:], func=AF.Square),
                     reads=(pk,), writes=(("tmp", j % 2),))
                P.op("dve", lambda e: e.scalar_tensor_tensor(out=aT[:, j, :], in0=pt[:], scalar=0.0, in1=tmp[:, j % 2, :],
                                                             op0=ALU.is_gt, op1=ALU.mult),
                     reads=(pk, ("tmp", j % 2)), writes=(("a", j),))
            E.gemm(w_up, KC, lambda k: hT[:, k, :], lambda k: ("h", k), 0, 4 * D, epi_up)

            def epi_dn(j, pt, pk):
                P.op("dve", lambda e: e.scalar_tensor_tensor(out=xT[:, j, :], in0=pt[:], scalar=modT[:, 80 + j:81 + j],
                                                             in1=xT[:, j, :], op0=ALU.mult, op1=ALU.add),
                     reads=(pk, "mod", ("x", j)), writes=(("x", j),))
                P.dma("sp", yT_d[j * 128:(j + 1) * 128, t0:t0 + TB], xT[:, j, :], reads=(("x", j),))
            E.gemm(w_down, 64, lambda k: aT[:, k, :], lambda k: ("a", k), 0, D, epi_dn)
        P.emit()
    return nc

import math
import numpy as np
from contextlib import ExitStack
import concourse.bass as bass
import concourse.mybir as mybir

N_IN = 8784
PI = math.pi


def l1_consts():
    c = np.zeros((128, 4), np.float32)
    d = np.arange(128)
    c[:, 0] = 10000.0 ** (-(d % 64) * 2.0 / 128)
    c[:, 1] = np.where(d < 64, -1.0, 1.0)
    c[:, 2] = 10000.0 ** (-(d % 32) * 2.0 / 64)
    c[:, 3] = np.where((d % 64) < 32, -1.0, 1.0)
    p128 = np.zeros((128, 128), np.float32)
    p64 = np.zeros((128, 128), np.float32)
    for i in range(128):
        p128[(i + 64) % 128, i] = 1.0
        p64[(i // 64) * 64 + ((i % 64) + 32) % 64, i] = 1.0
    return c, p128, p64


def build_L1(T):
    nc = bass.Bass("TRN2", target_bir_lowering=False)
    NB = T // TB
    dt = lambda n, s, d, k="ExternalInput": nc.dram_tensor(n, s, d, kind=k).ap()
    xT_d = dt("xT", [D, T], F32)
    c_d = dt("c", [128, KC], F32)
    w_ada = dt("w_ada", [D, 6 * D], F32)
    b_ada = dt("b_ada", [128, 96], F32)
    w_in = dt("w_in", [D, N_IN], F32)
    gains_d = dt("gains", [128, 5], F32)
    pos_d = dt("pos", [1, T], I32)
    rc_d = dt("ropec", [128, 4], F32)
    p128_d = dt("p128", [128, 128], F32)
    p64_d = dt("p64", [128, 128], F32)
    BF = BF16
    qk_o = dt("qkT", [40, 128, T], BF, "ExternalOutput")
    v_o = dt("v", [T, 2560], BF, "ExternalOutput")
    iq_o = dt("iqT", [8, 128, T], BF, "ExternalOutput")
    ikw_o = dt("ikw", [80, T], BF, "ExternalOutput")
    mod_o = dt("modo", [128, 96], F32, "ExternalOutput")

    with ExitStack() as st:
        P = Prog(nc, st)
        E = Env(P)
        modT = P.sb([128, 96], F32, "modT")
        compute_mod(P, E, c_d, w_ada, b_ada, 0, 96, modT)
        P.dma("sp", mod_o, modT[:], reads=("mod",))
        scp = P.sb([128, KC], F32, "scp")
        P.op("dve", lambda e: e.tensor_scalar(out=scp[:], in0=modT[:, 16:32], scalar1=1.0, scalar2=None, op0=ALU.add),
             reads=("mod",), writes=("modp",))
        eps = P.sb([128, 1], F32, "eps")
        P.op("dve", lambda e: e.memset(eps[:], EPS), writes=("eps",))
        ones = P.sb([128, 128], BF16, "ones")
        P.op("dve", lambda e: e.memset(ones[:], 1.0), writes=("ones",))
        gains = P.sb([128, 5], F32, "gains")
        P.dma("sp", gains[:], gains_d, writes=("gains",))
        rc = P.sb([128, 4], F32, "rc")
        P.dma("sp", rc[:], rc_d, writes=("rc",))
        pf = P.sb([128, 2, 128], F32, "pf")
        pm = P.sb([128, 2, 128], BF16, "pm")
        P.dma("sp", pf[:, 0, :], p128_d, writes=("pf",))
        P.dma("sp", pf[:, 1, :], p64_d, writes=("pf",))
        P.op("dve", lambda e: e.tensor_copy(out=pm[:], in_=pf[:]), reads=("pf",), writes=("pm",))

        xT = P.sb([128, KC, TB], F32, "xT")
        hT = P.sb([128, KC, TB], BF16, "hT")
        sq = P.sb([128, KC, TB], BF16, "sq")
        rstd = P.sb([128, TB], F32, "rstd")
        tmp = P.sb([128, 2, TB], F32, "tmp")
        pss = P.ps([128, 512], F32, "pss")
        psw = P.ps([128, 512], F32, "psw")
        posi = P.sb([128, TB], I32, "posi")
        posf = P.sb([128, TB], F32, "posf")
        tab = P.sb([128, 4, TB], F32, "tab")
        ang = P.sb([128, TB], F32, "ang")
        kf = P.sb([128, TB], F32, "kf")
        ki = P.sb([128, TB], I32, "ki")
        NR = 3
        zg = P.sb([128, NR, TB], BF16, "zg")
        sqz = P.sb([128, NR, TB], BF16, "sqz")
        t1 = P.sb([128, NR, TB], F32, "t1")
        rs = P.sb([128, NR, TB], F32, "rs")
        stg = P.sb([128, NR, TB], BF16, "stg")
        rr = [0]

        def make_table(ti, fcol, scol, shift):
            P.op("dve", lambda e: e.tensor_scalar(out=ang[:], in0=posf[:], scalar1=rc[:, fcol:fcol + 1], scalar2=shift,
                                                  op0=ALU.mult, op1=ALU.add), reads=("posf", "rc"), writes=("ang",))
            P.op("dve", lambda e: e.tensor_scalar(out=kf[:], in0=ang[:], scalar1=1.0 / (2 * PI), scalar2=0.5,
                                                  op0=ALU.mult, op1=ALU.add), reads=("ang",), writes=("kf",))
            P.op("dve", lambda e: e.tensor_copy(out=ki[:], in_=kf[:]), reads=("kf",), writes=("ki",))
            P.op("dve", lambda e: e.tensor_copy(out=kf[:], in_=ki[:]), reads=("ki",), writes=("kf",))
            P.op("dve", lambda e: e.scalar_tensor_tensor(out=ang[:], in0=kf[:], scalar=-2 * PI, in1=ang[:],
                                                         op0=ALU.mult, op1=ALU.add), reads=("kf", "ang"), writes=("ang",))
            P.op("dve", lambda e: e.tensor_scalar(out=kf[:], in0=ang[:], scalar1=-PI, scalar2=2 * PI,
                                                  op0=ALU.is_lt, op1=ALU.mult), reads=("ang",), writes=("kf",))
            P.op("dve", lambda e: e.tensor_tensor(out=ang[:], in0=ang[:], in1=kf[:], op=ALU.add), reads=("ang", "kf"), writes=("ang",))
            P.op("dve", lambda e: e.tensor_scalar(out=kf[:], in0=ang[:], scalar1=PI, scalar2=-2 * PI,
                                                  op0=ALU.is_gt, op1=ALU.mult), reads=("ang",), writes=("kf",))
            P.op("dve", lambda e: e.tensor_tensor(out=ang[:], in0=ang[:], in1=kf[:], op=ALU.add), reads=("ang", "kf"), writes=("ang",))
            P.op("dve", lambda e: e.tensor_scalar(out=ang[:], in0=ang[:], scalar1=-3.1415925, scalar2=3.1415925,
                                                  op0=ALU.max, op1=ALU.min), reads=("ang",), writes=("ang",))
            if scol is None:
                P.op("act", lambda e: e.activation(out=tab[:, ti, :], in_=ang[:], func=AF.Sin), reads=("ang",), writes=(("tab", ti),))
            else:
                P.op("act", lambda e: e.activation(out=tab[:, ti, :], in_=ang[:], func=AF.Sin, scale=rc[:, scol:scol + 1]),
                     reads=("ang", "rc"), writes=(("tab", ti),))

        for b in range(NB):
            t0 = b * TB
            for k in range(KC):
                P.dma("sp", xT[:, k, :], xT_d[k * 128:(k + 1) * 128, t0:t0 + TB], writes=(("x", k),))
            P.dma("sp", posi[:], pos_d[0:1, t0:t0 + TB].partition_broadcast(128), writes=("posi",))
            P.op("dve", lambda e: e.tensor_copy(out=posf[:], in_=posi[:]), reads=("posi",), writes=("posf",))
            make_table(0, 0, None, PI / 2)
            make_table(1, 0, 1, 0.0)
            make_table(2, 2, None, PI / 2)
            make_table(3, 2, 3, 0.0)
            rms_mod(P, xT, "x", hT, "h", sq, (lambda k: ("sq", k)), ones, pss, rstd, tmp, eps, scp, modT[:, 0:16])

            def rope_out(pt, pk, rows, ci, si, pmi, gcol, norm, dst):
                r = rr[0] % NR
                rr[0] += 1
                R = slice(0, rows)
                if gcol is None:
                    P.op("act", lambda e: e.activation(out=zg[R, r, :], in_=pt[R, :], func=AF.Copy),
                         reads=(pk,), writes=(("zg", r),))
                else:
                    P.op("act", lambda e: e.activation(out=zg[R, r, :], in_=pt[R, :], func=AF.Identity, scale=gains[R, gcol:gcol + 1]),
                         reads=(pk, "gains"), writes=(("zg", r),))
                if norm:
                    P.op("act", lambda e: e.activation(out=sqz[R, r, :], in_=pt[R, :], func=AF.Square),
                         reads=(pk,), writes=(("sqz", r),))
                    P.op("pe", lambda e: e.matmul(pss[R, :], ones[R, R], sqz[R, r, :], start=True, stop=True),
                         reads=(("sqz", r), "ones"), writes=("pss",))
                    P.op("act", lambda e: e.activation(out=rs[R, r, :], in_=pss[R, :], func=AF.Sqrt, scale=1.0 / rows, bias=eps[R, 0:1]),
                         reads=("pss", "eps"), writes=(("rs", r),))
                    P.op("dve", lambda e: e.reciprocal(out=rs[R, r, :], in_=rs[R, r, :]), reads=(("rs", r),), writes=(("rs", r),))
                P.op("pe", lambda e: e.matmul(psw[R, :], pm[R, pmi, R], zg[R, r, :], start=True, stop=True),
                     reads=(("zg", r), "pm"), writes=("psw",))
                P.op("pool", lambda e: e.tensor_tensor(out=t1[R, r, :], in0=zg[R, r, :], in1=tab[R, ci, :], op=ALU.mult),
                     reads=(("zg", r), ("tab", ci)), writes=(("t1", r),))
                P.op("dve", lambda e: e.tensor_tensor(out=tmp[R, r % 2, :], in0=psw[R, :], in1=tab[R, si, :], op=ALU.mult),
                     reads=("psw", ("tab", si)), writes=(("tmp", r % 2),))
                if norm:
                    P.op("dve", lambda e: e.tensor_tensor(out=t1[R, r, :], in0=t1[R, r, :], in1=tmp[R, r % 2, :], op=ALU.add),
                         reads=(("t1", r), ("tmp", r % 2)), writes=(("t1", r),))
                    P.op("dve", lambda e: e.tensor_tensor(out=stg[R, r, :], in0=t1[R, r, :], in1=rs[R, r, :], op=ALU.mult),
                         reads=(("t1", r), ("rs", r)), writes=(("stg", r),))
                else:
                    P.op("dve", lambda e: e.tensor_tensor(out=stg[R, r, :], in0=t1[R, r, :], in1=tmp[R, r % 2, :], op=ALU.add),
                         reads=(("t1", r), ("tmp", r % 2)), writes=(("stg", r),))
                P.dma("sp", dst, stg[R, r, :], reads=(("stg", r),))

            hk = lambda k: ("h", k)
            hr = lambda k: hT[:, k, :]
            for (c0, nh, gcol, o0) in ((0, 12, 0, 0), (1536, 12, 1, 12), (4608, 8, 2, 24), (5632, 8, 3, 32)):
                def epi(j, pt, pk, gcol=gcol, o0=o0):
                    rope_out(pt, pk, 128, 0, 1, 0, gcol, True, qk_o[o0 + j, :, t0:t0 + TB])
                E.gemm(w_in, KC, hr, hk, c0, nh * 128, epi)
            def epi_iq(j, pt, pk):
                rope_out(pt, pk, 128, 2, 3, 1, None, False, iq_o[j, :, t0:t0 + TB])
            E.gemm(w_in, KC, hr, hk, 7680, 1024, epi_iq)
            def epi_ik(j, pt, pk):
                rope_out(pt, pk, 64, 2, 3, 1, 4, True, ikw_o[0:64, t0:t0 + TB])
                r = rr[0] % NR
                rr[0] += 1
                P.op("act", lambda e: e.activation(out=stg[64:80, r, :], in_=pt[64:80, :], func=AF.Copy, scale=0.25),
                     reads=(pk,), writes=(("stg", r),))
                P.dma("sp", ikw_o[64:80, t0:t0 + TB], stg[64:80, r, :], reads=(("stg", r),))
            E.gemm(w_in, KC, hr, hk, 8704, 80, epi_ik)
            for (c0, ncols, o0) in ((3072, 1536, 0), (6656, 1024, 1536)):
                for ct in range(ncols // WC):
                    wt, wk = E.wtile(w_in, 0, KC, c0 + ct * WC)
                    for tt in range(4):
                        pt, pk = E.bank()
                        for k in range(KC):
                            P.op("pe", lambda e, pt=pt, wt=wt, k=k, tt=tt: e.matmul(
                                pt[:, 0:WC], hT[:, k, tt * 128:(tt + 1) * 128], wt[:, k, :], start=(k == 0), stop=(k == KC - 1)),
                                reads=(wk, ("h", k)), writes=(pk,))
                        r = rr[0] % NR
                        rr[0] += 1
                        P.op("act", lambda e, pt=pt, r=r: e.activation(out=stg[:, r, 0:WC], in_=pt[:, 0:WC], func=AF.Copy),
                             reads=(pk,), writes=(("stg", r),))
                        P.dma("sp", v_o[t0 + tt * 128:t0 + (tt + 1) * 128, o0 + ct * WC:o0 + (ct + 1) * WC], stg[:, r, 0:WC],
                              reads=(("stg", r),))
        P.emit()
    return nc

import math
import numpy as np
import ml_dtypes
from contextlib import ExitStack
import concourse.bass as bass
import concourse.mybir as mybir

FP8 = mybir.dt.float8e4
AX = mybir.AxisListType
bf = ml_dtypes.bfloat16
GW = (128, 512, 2048)
GR = (1, 4, 16)
GWT = (1, 4, 16)
GLEN = (640, 1024, 2560)
GOFF = (0, 640, 1664)
GNT = (5, 8, 20)
GTOFF = (0, 5, 13)
GM0 = (0, 2, 7)
NITER = 26
TOPK = 256
NEG = -3.0e38


def l2_consts():
    cmA = np.zeros((128, 24, 128), np.float32)
    k = np.arange(128)[:, None]
    q = np.arange(128)[None, :]
    for g in range(3):
        for dl in range(GWT[g] + 1):
            dist = 128 * dl + q - k
            cmA[:, GM0[g] + dl, :] = ((dist >= 0) & (dist <= GW[g]) & (dist % GR[g] == 0))
    pow2 = np.tile((0.5 ** (np.arange(NITER) + 1)).astype(np.float32)[None, :], (128, 1))
    ident = np.eye(128, dtype=np.float32)
    return cmA.astype(bf), pow2, ident.astype(bf)


def l2_inputs(QK, V, IQ, IKW, r, NBLK):
    S = QK.shape[2]
    T = NBLK * 512
    tok = np.concatenate([np.arange(512) + 512 * (4 * i + r) for i in range(NBLK)])
    PAD = 2048
    KApad = np.zeros((12, 128, PAD + S), QK.dtype)
    KApad[:, :, PAD:] = QK[12:24]
    VApad = np.zeros((PAD + S, 1536), V.dtype)
    VApad[PAD:] = V[:, :1536]
    valid = np.zeros((PAD + S,), V.dtype)
    valid[PAD:] = 1
    qa = np.zeros((NBLK, 4, 128, 3, 512), QK.dtype)
    kaw = np.zeros((NBLK, 4, 128, 4224), QK.dtype)
    vaw = np.zeros((NBLK, 4, 128, 33, 129), QK.dtype)
    for i in range(NBLK):
        tb = 512 * (4 * i + r)
        for hs in range(4):
            for g in range(3):
                h = g * 4 + hs
                qa[i, hs, :, g, :] = QK[h, :, tb:tb + 512]
                w0 = PAD + tb - GW[g]
                kaw[i, hs, :, GOFF[g]:GOFF[g] + GLEN[g]] = KApad[h, :, w0:w0 + GLEN[g]]
                vv = VApad[w0:w0 + GLEN[g], h * 128:(h + 1) * 128].reshape(GNT[g], 128, 128)
                vaw[i, hs, :, GTOFF[g]:GTOFF[g] + GNT[g], 0:128] = vv.transpose(1, 0, 2)
                vaw[i, hs, :, GTOFF[g]:GTOFF[g] + GNT[g], 128] = valid[w0:w0 + GLEN[g]].reshape(GNT[g], 128).T
    NQT = T // 128
    qb = QK[24:32][:, :, tok].reshape(8, 128, NQT, 128).transpose(2, 1, 0, 3)
    iqz = np.zeros((NQT, 128, 8, 2, 128), QK.dtype)
    iql = IQ[:, :, tok].reshape(8, 128, NQT, 128)
    iqz[:, 0:64, :, 0, :] = iql[:, 0:64].transpose(2, 1, 0, 3)
    iqz[:, 64:128, :, 1, :] = iql[:, 64:128].transpose(2, 1, 0, 3)
    iqz = iqz.reshape(NQT, 128, 8, 2, 16, 8).transpose(0, 1, 4, 2, 3, 5).reshape(NQT, 128, 16, 128)
    iw = IKW[64:80][:, tok].reshape(16, NQT, 128)
    wsel = np.zeros((NQT, 16, 8, 16, 128), QK.dtype)
    for g in range(16):
        for qs in range(8):
            wsel[:, :, qs, g, 8 * g + qs] = iw[:, :, 8 * g + qs].T
    wsel = wsel.reshape(NQT, 128, 16, 128)
    ik2 = np.concatenate([IKW[0:64], IKW[0:64]], 0)
    NC = S // 256
    kbc = QK[32:40].reshape(8, 128, NC, 256).transpose(2, 1, 0, 3)
    vbc = V[:, 1536:].reshape(NC, 2, 128, 1024).transpose(0, 2, 1, 3)
    cm = np.zeros((4, 128, 2048), np.float32)
    sp = np.arange(2048)[None, :]
    for u in range(4):
        tq = 512 * r + 128 * u + np.arange(128)[:, None]
        cm[u] = np.where(sp <= tq, 0.0, NEG)
    cmA, pow2, ident = l2_consts()
    c = np.ascontiguousarray
    return dict(qa=c(qa), kaw=c(kaw), vaw=c(vaw), qb=c(qb), iqz=c(iqz), wsel=c(wsel), ik2=c(ik2), kbc=c(kbc),
                vbc=c(vbc), cm=c(cm.astype(bf)), cmA=cmA, pow2=pow2, ident=ident)


def build_L2(NBLK, S, phases=('A','I','S','B')):
    nc = bass.Bass("TRN2", target_bir_lowering=False)
    T = NBLK * 512
    NQT = T // 128
    dt = lambda n, s, d, k="ExternalInput": nc.dram_tensor(n, s, d, kind=k).ap()
    qa_d = dt("qa", [NBLK, 4, 128, 3, 512], BF16)
    kaw_d = dt("kaw", [NBLK, 4, 128, 4224], BF16)
    vaw_d = dt("vaw", [NBLK, 4, 128, 33, 129], BF16)
    qb_d = dt("qb", [NQT, 128, 8, 128], BF16)
    iqz_d = dt("iqz", [NQT, 128, 16, 128], BF16)
    wsel_d = dt("wsel", [NQT, 128, 16, 128], BF16)
    ik2_d = dt("ik2", [128, S], BF16)
    kbc_d = dt("kbc", [S // 256, 128, 8, 256], BF16)
    vbc_d = dt("vbc", [S // 256, 128, 2, 1024], BF16)
    cm_d = dt("cm", [4, 128, 2048], BF16)
    cmA_d = dt("cmA", [128, 24, 128], BF16)
    pow2_d = dt("pow2", [128, NITER], F32)
    ident_d = dt("ident", [128, 128], BF16)
    oa_o = dt("oa", [T, 512], BF16, "ExternalOutput")
    ob_o = dt("ob", [T, 1024], BF16, "ExternalOutput")
    SCL = 128 ** -0.5

    with ExitStack() as st:
        P = Prog(nc, st)
        ring = [(P.ps([128, 512], F32, "rg%d" % i), "rg%d" % i) for i in range(4)]
        ri = [0]

        def bank():
            b = ring[ri[0] % 4]
            ri[0] += 1
            return b
        SC = P.ps([128, 512], F32, "SC")
        OB = [P.ps([128, 512], F32, "OB%d" % i) for i in range(2)]
        DEN = P.ps([128, 512], F32, "DEN")

        sc = P.sb([128, S], F32, "sc")
        mk = [P.sb([128, S], FP8, "mk%d" % i) for i in range(2)]
        NKC = 3
        kc = [P.sb([128, 8, 256], BF16, "kc%d" % i) for i in range(NKC)]
        vc = [P.sb([128, 2, 1024], BF16, "vc%d" % i) for i in range(NKC)]
        ikr = [P.sb([128, 512], BF16, "ik%d" % i) for i in range(3)]
        Rr = [P.sb([128, 512], BF16, "R%d" % i) for i in range(4)]
        pTr = [P.sb([128, 4, 128], BF16, "pT%d" % i) for i in range(4)]
        mTr = [P.sb([128, 2, 128], BF16, "mT%d" % i) for i in range(3)]
        iqz = [P.sb([128, 16, 128], BF16, "iqz%d" % i) for i in range(2)]
        wsel = [P.sb([128, 16, 128], BF16, "wsel%d" % i) for i in range(2)]
        qb = [P.sb([128, 8, 128], BF16, "qb%d" % i) for i in range(2)]
        cm = P.sb([128, 2048], BF16, "cm")
        stg = P.sb([128, 1024], BF16, "stg")
        stga = P.sb([128, 4, 128], BF16, "stga")
        qa = P.sb([128, 3, 512], BF16, "qa")
        kaw = P.sb([128, 4224], BF16, "kaw")
        vaw = P.sb([128, 33, 129], BF16, "vaw")
        cmA = P.sb([128, 24, 128], BF16, "cmA")
        pow2 = P.sb([128, NITER], F32, "pow2")
        ident = P.sb([128, 128], BF16, "ident")
        onec = P.sb([128, 1], BF16, "onec")
        sm = P.sb([128, 64], F32, "sm")
        sk = P.sb([128, NITER], F32, "sk")
        rden = P.sb([128, 8], F32, "rden")
        P.dma("sp", cmA[:], cmA_d, writes=("cmA",))
        P.dma("sp", pow2[:], pow2_d, writes=("pow2",))
        P.dma("sp", ident[:], ident_d, writes=("ident",))
        P.op("dve", lambda e: e.memset(onec[:], 1.0), writes=("onec",))
        cnt = {"ik": 0, "R": 0, "pT": 0, "mT": 0, "kc": 0}

        def nxt(name, n):
            i = cnt[name] % n
            cnt[name] += 1
            return i

        def mixA(i):
            for hs in range(4):
                P.dma("sp", qa[:], qa_d[i, hs], writes=("qa",))
                P.dma("sp", kaw[:], kaw_d[i, hs], writes=("kaw",))
                P.dma("sp", vaw[:], vaw_d[i, hs], writes=("vaw",))
                first = True
                items = [(g, dl) for g in range(3) for dl in range(GWT[g] + 1)]
                nmm = 4 * len(items)
                done = 0
                for u in range(4):
                    for b0 in range(0, len(items), 4):
                        Sb, Sk = bank()
                        for b4 in range(4):
                            g, dl = items[b0 + b4]
                            tile = GWT[g] + u - dl
                            kcol = GOFF[g] + tile * 128
                            P.op("pe", lambda e, Sb=Sb, b4=b4, kcol=kcol, g=g, u=u: e.matmul(
                                Sb[:, b4 * 128:(b4 + 1) * 128], kaw[:, kcol:kcol + 128], qa[:, g, u * 128:(u + 1) * 128],
                                start=True, stop=True), reads=("kaw", "qa"), writes=(Sk,))
                        pi = nxt("pT", 4)
                        pT = pTr[pi]
                        P.op("act", lambda e, pT=pT, Sb=Sb: e.activation(out=pT[:].rearrange("p a b -> p (a b)"), in_=Sb[:], func=AF.Exp, scale=SCL),
                             reads=(Sk,), writes=(("pT", pi),))
                        P.op("pool", lambda e, pT=pT, b0=b0: e.tensor_tensor(out=pT[:], in0=pT[:], in1=cmA[:, b0:b0 + 4, :], op=ALU.mult),
                             reads=(("pT", pi), "cmA"), writes=(("pT", pi),))
                        for b4 in range(4):
                            g, dl = items[b0 + b4]
                            tile = GTOFF[g] + GWT[g] + u - dl
                            done += 1
                            last = (done == nmm)
                            P.op("pe", lambda e, pT=pT, b4=b4, tile=tile, u=u, first=first, last=last: e.matmul(
                                OB[0][:, u * 128:(u + 1) * 128], pT[:, b4, :], vaw[:, tile, 0:128], start=first, stop=last),
                                reads=(("pT", pi), "vaw"), writes=("OB0",))
                            P.op("pe", lambda e, pT=pT, b4=b4, tile=tile, u=u, first=first, last=last: e.matmul(
                                DEN[:, u:u + 1], pT[:, b4, :], vaw[:, tile, 128:129], start=first, stop=last),
                                reads=(("pT", pi), "vaw"), writes=("DEN",))
                            first = False
                P.op("dve", lambda e: e.reciprocal(out=rden[:, 0:4], in_=DEN[:, 0:4]), reads=("DEN",), writes=("rden",))
                for u in range(4):
                    P.op("act", lambda e, u=u: e.activation(out=stga[:, u, :], in_=OB[0][:, u * 128:(u + 1) * 128], func=AF.Identity,
                                                         scale=rden[:, u:u + 1]), reads=("OB0", "rden"), writes=("stga",))
                P.dma("sp", oa_o[512 * i:512 * i + 512, hs * 128:(hs + 1) * 128].rearrange("(u q) d -> q u d", q=128), stga[:],
                      reads=("stga",))

        def A1(qt, N):
            u = qt % 4
            z = iqz[qt % 2]
            w = wsel[qt % 2]
            zk, wk = ("iqz", qt % 2), ("wsel", qt % 2)
            P.dma("sp", z[:], iqz_d[qt], writes=(zk,))
            P.dma("sp", w[:], wsel_d[qt], writes=(wk,))
            P.dma("sp", cm[:], cm_d[u], writes=("cm",))
            for c in range(N // 512):
                ii = nxt("ik", 3)
                P.dma("sp", ikr[ii][:], ik2_d[:, c * 512:(c + 1) * 512], writes=(("ik", ii),))
                for g in range(16):
                    Lb, Lk = bank()
                    P.op("pe", lambda e, Lb=Lb, g=g, ii=ii: e.matmul(Lb[:], z[:, g, :], ikr[ii][:], start=True, stop=True),
                         reads=(zk, ("ik", ii)), writes=(Lk,))
                    r = nxt("R", 4)
                    P.op("act", lambda e, Lb=Lb, r=r: e.activation(out=Rr[r][:], in_=Lb[:], func=AF.Relu),
                         reads=(Lk,), writes=(("R", r),))
                    P.op("pe", lambda e, g=g, r=r: e.matmul(SC[:], w[:, g, :], Rr[r][:], start=(g == 0), stop=(g == 15)),
                         reads=(wk, ("R", r)), writes=("SC",))
                P.op("dve", lambda e, c=c: e.tensor_copy(out=sc[:, c * 512:(c + 1) * 512], in_=SC[:]),
                     reads=("SC",), writes=(("sc", c),))

        def bisect(qt, N):
            m = mk[qt % 2]
            mkey = ("mk", qt % 2)
            nch = N // 512
            allsc = tuple(("sc", c) for c in range(nch))
            A, w0, lo, mid, cn, tt = (sm[:, i:i + 1] for i in range(6))
            P.op("dve", lambda e: e.tensor_reduce(out=A, in_=sc[:, 0:N], op=ALU.max, axis=AX.X, apply_absolute_value=True),
                 reads=allsc, writes=("A",))
            P.op("dve", lambda e: e.tensor_tensor(out=sc[:, N - 2048:N], in0=sc[:, N - 2048:N], in1=cm[:], op=ALU.add),
                 reads=allsc[-4:] + ("cm", "A"), writes=allsc[-4:])
            P.op("dve", lambda e: e.tensor_scalar(out=w0, in0=A, scalar1=2.0, scalar2=2.0, op0=ALU.mult, op1=ALU.add),
                 reads=("A",), writes=("w0",))
            P.op("dve", lambda e: e.tensor_scalar(out=lo, in0=A, scalar1=-1.0, scalar2=-1.0, op0=ALU.mult, op1=ALU.add),
                 reads=("A",), writes=("lo",))
            P.op("dve", lambda e: e.tensor_scalar(out=sk[:], in0=pow2[:], scalar1=w0, scalar2=None, op0=ALU.mult),
                 reads=("w0", "pow2"), writes=("sk",))
            for k in range(NITER):
                P.op("dve", lambda e, k=k: e.tensor_tensor(out=mid, in0=lo, in1=sk[:, k:k + 1], op=ALU.add),
                     reads=("lo", "sk"), writes=("mid",))
                P.op("dve", lambda e: e.tensor_scalar(out=m[:, 0:N], in0=sc[:, 0:N], scalar1=mid, scalar2=None, op0=ALU.is_ge,
                                                      op1=ALU.add, accum_out=cn, saturate=False),
                     reads=allsc + ("mid",), writes=(mkey, "cn"))
                P.op("dve", lambda e, k=k: e.tensor_scalar(out=tt, in0=cn, scalar1=float(TOPK), scalar2=sk[:, k:k + 1],
                                                           op0=ALU.is_ge, op1=ALU.mult), reads=("cn", "sk"), writes=("tt",))
                P.op("dve", lambda e: e.tensor_tensor(out=lo, in0=lo, in1=tt, op=ALU.add), reads=("lo", "tt"), writes=("lo",))
            P.op("dve", lambda e: e.tensor_scalar(out=m[:, 0:N], in0=sc[:, 0:N], scalar1=lo, scalar2=None, op0=ALU.is_ge, saturate=False),
                 reads=allsc + ("lo",), writes=(mkey,))

        def B(qt, N):
            m = mk[qt % 2]
            mkey = ("mk", qt % 2)
            q = qb[qt % 2]
            qk = ("qb", qt % 2)
            P.dma("sp", q[:], qb_d[qt], writes=(qk,))
            nc_ = N // 256
            first = True
            for c in range(nc_):
                ci = nxt("kc", NKC)
                P.dma("sp", kc[ci][:], kbc_d[c], writes=(("kc", ci),))
                P.dma("sp", vc[ci][:], vbc_d[c], writes=(("vc", ci),))
                Tb, Tk = bank()
                for kt in range(2):
                    P.op("pe", lambda e, Tb=Tb, kt=kt, c=c: e.matmul(Tb[:, kt * 128:(kt + 1) * 128],
                                                                   m[:, c * 256 + kt * 128:c * 256 + (kt + 1) * 128], ident[:],
                                                                   start=True, stop=True),
                         reads=(mkey, "ident"), writes=(Tk,))
                mi = nxt("mT", 3)
                P.op("act", lambda e, Tb=Tb, mi=mi: e.activation(out=mTr[mi][:].rearrange("p a b -> p (a b)"), in_=Tb[:, 0:256], func=AF.Copy),
                     reads=(Tk,), writes=(("mT", mi),))
                for kt in range(2):
                    for hg in range(2):
                        Sb, Sk = bank()
                        for hh in range(4):
                            h = hg * 4 + hh
                            P.op("pe", lambda e, Sb=Sb, hh=hh, h=h, kt=kt, ci=ci: e.matmul(
                                Sb[:, hh * 128:(hh + 1) * 128], kc[ci][:, h, kt * 128:(kt + 1) * 128], q[:, h, :], start=True, stop=True),
                                reads=(("kc", ci), qk), writes=(Sk,))
                        pi = nxt("pT", 4)
                        pT = pTr[pi]
                        P.op("act", lambda e, pT=pT, Sb=Sb: e.activation(out=pT[:].rearrange("p a b -> p (a b)"), in_=Sb[:], func=AF.Exp, scale=SCL),
                             reads=(Sk,), writes=(("pT", pi),))
                        P.op("pool", lambda e, pT=pT, mi=mi, kt=kt: e.tensor_tensor(
                            out=pT[:], in0=pT[:], in1=mTr[mi][:, kt, :].unsqueeze(1).to_broadcast([128, 4, 128]), op=ALU.mult),
                            reads=(("pT", pi), ("mT", mi)), writes=(("pT", pi),))
                        last = (c == nc_ - 1 and kt == 1)
                        for hh in range(4):
                            h = hg * 4 + hh
                            fo = first and hh == 0
                            fd = first and hg == 0 and hh == 0
                            P.op("pe", lambda e, pT=pT, hh=hh, h=h, kt=kt, ci=ci, hg=hg, fo=fo, last=last: e.matmul(
                                OB[hg][:, hh * 128:(hh + 1) * 128], pT[:, hh, :], vc[ci][:, kt, h * 128:(h + 1) * 128],
                                start=fo, stop=(last and hh == 3)), reads=(("pT", pi), ("vc", ci)), writes=("OB%d" % hg,))
                            P.op("pe", lambda e, pT=pT, hh=hh, h=h, fd=fd, last=last, hg=hg: e.matmul(
                                DEN[:, h:h + 1], pT[:, hh, :], onec[:, 0:1], start=fd, stop=(last and hg == 1 and hh == 3)),
                                reads=(("pT", pi), "onec"), writes=("DEN",))
                    first = False
            P.op("dve", lambda e: e.reciprocal(out=rden[:, 0:8], in_=DEN[:, 0:8]), reads=("DEN",), writes=("rden",))
            for h in range(8):
                P.op("act", lambda e, h=h: e.activation(out=stg[:, h * 128:(h + 1) * 128], in_=OB[h // 4][:, (h % 4) * 128:(h % 4 + 1) * 128],
                                                     func=AF.Identity, scale=rden[:, h:h + 1]),
                     reads=("OB%d" % (h // 4), "rden"), writes=("stg",))
            P.dma("sp", ob_o[qt * 128:(qt + 1) * 128, :], stg[:], reads=("stg",))

        prev = None
        for i in range(NBLK):
            if 'A' in phases:
                mixA(i)
            for u in range(4):
                qt = 4 * i + u
                N = 2048 * (i + 1)
                if 'I' in phases:
                    A1(qt, N)
                if 'S' in phases:
                    bisect(qt, N)
                if prev is not None and 'B' in phases:
                    B(*prev)
                prev = (qt, N)
        if 'B' in phases:
            B(*prev)
        P.emit()
    return nc

from concourse.bass_utils import run_bass_kernel_spmd

NCORES = 8
SEQ = 16384
NBLK = 8
TCORE = NBLK * 512
_CACHE = {}


def _prog(name, fn):
    if name not in _CACHE:
        _CACHE[name] = fn()
    return _CACHE[name]


def _tok(r):
    return np.concatenate([np.arange(512) + 512 * (4 * i + r) for i in range(NBLK)])


def kernel(x, c, positions, w_ada, b_ada, w_in, a_q_gain, a_k_gain, b_q_gain, b_k_gain, idx_k_gain,
           w_gate, b_gate, w_proj_a, w_proj_b, w_out, w_up, w_down):
    f32 = np.float32
    x = np.asarray(x, f32)
    depth = w_in.shape[0]
    cores = list(range(NCORES))
    toks = [_tok(r) for r in range(4)]
    ca = np.ascontiguousarray
    xT = [ca(x[cid // 4][toks[cid % 4]].T) for cid in cores]
    pos = [ca(np.asarray(positions)[cid // 4][toks[cid % 4]].reshape(1, TCORE).astype(np.int32)) for cid in cores]
    ccol = [ca(np.asarray(c, f32)[cid // 4].reshape(16, 128).T) for cid in cores]
    rc, p128, p64 = l1_consts()
    for l in range(depth):
        gains = np.zeros((128, 5), f32)
        for i, g in enumerate((a_q_gain, a_k_gain, b_q_gain, b_k_gain)):
            gains[:, i] = np.asarray(g, f32)[l]
        gains[:64, 4] = np.asarray(idx_k_gain, f32)[l]
        bada = ca(np.asarray(b_ada, f32)[l].reshape(96, 128).T)
        wada = ca(np.asarray(w_ada, f32)[l])
        win = ca(np.asarray(w_in, f32)[l])
        nc1 = _prog("l1", lambda: build_L1(TCORE))
        ins = [dict(xT=xT[cid], c=ccol[cid], w_ada=wada, b_ada=bada, w_in=win, gains=gains, pos=pos[cid],
                    ropec=rc, p128=p128, p64=p64) for cid in cores]
        r1 = run_bass_kernel_spmd(nc1, ins, core_ids=cores).results
        mods = [r1[cid]["modo"] for cid in cores]
        ins2 = []
        for b in range(2):
            QK = np.zeros((40, 128, SEQ), r1[0]["qkT"].dtype)
            V = np.zeros((SEQ, 2560), r1[0]["v"].dtype)
            IQ = np.zeros((8, 128, SEQ), r1[0]["iqT"].dtype)
            IKW = np.zeros((80, SEQ), r1[0]["ikw"].dtype)
            for r in range(4):
                o = r1[b * 4 + r]
                QK[:, :, toks[r]] = o["qkT"]
                V[toks[r]] = o["v"]
                IQ[:, :, toks[r]] = o["iqT"]
                IKW[:, toks[r]] = o["ikw"]
            for r in range(4):
                ins2.append(l2_inputs(QK, V, IQ, IKW, r, NBLK))
        del r1
        nc2 = _prog("l2", lambda: build_L2(NBLK, SEQ))
        r2 = run_bass_kernel_spmd(nc2, ins2, core_ids=cores).results
        del ins2
        oT = [ca(np.concatenate([r2[cid]["oa"], r2[cid]["ob"]], axis=1).T) for cid in cores]
        del r2
        nc3 = _prog("l3", lambda: build_L3(TCORE))
        bg = ca(np.asarray(b_gate, f32)[l].reshape(32, 128).T)
        wl = {k: ca(np.asarray(v, f32)[l]) for k, v in (("w_gate", w_gate), ("w_proj_a", w_proj_a), ("w_proj_b", w_proj_b),
                                                         ("w_out", w_out), ("w_up", w_up), ("w_down", w_down))}
        ins3 = [dict(xT=xT[cid], oT=oT[cid], modi=mods[cid], b_gate=bg, **wl) for cid in cores]
        r3 = run_bass_kernel_spmd(nc3, ins3, core_ids=cores).results
        xT = [ca(r3[cid]["yT"]) for cid in cores]
        del r3, ins3
    out = np.zeros((2, SEQ, 2048), f32)
    for cid in cores:
        out[cid // 4][toks[cid % 4]] = xT[cid].T
    return out
```
